# Optimizing a Trainium2 kernel written in Bass

```python
import math
import jax, jax.numpy as jnp
from jax import lax
import numpy as np

D_MODEL = 4096
BATCH = 2
SEQ = 8192
DEPTH = 4

SSM_WIDTH = D_MODEL // 2
SSM_GROUP = 16
SSM_GROUPS = SSM_WIDTH // SSM_GROUP
SSM_STATE = 64
SCAN_CHUNK = 128
DT_MIN, DT_MAX = 1e-3, 1e-1
POOL_WIDTH = D_MODEL // 2
POOL_WINDOWS = (2, 4, 8, 16)
POOL_GROUPS = len(POOL_WINDOWS)
POOL_GROUP_WIDTH = POOL_WIDTH // POOL_GROUPS
IN_WIDTH = SSM_WIDTH + POOL_WIDTH + 2 * D_MODEL
D_FF = 2 * D_MODEL
CONV_WIDTH = 3
N_MOD = 6
RMS_EPS = 1e-6

kernel_name = "hybrid_s5_pool_gated_convffn_adaln"


def rmsnorm(x, g):
    xf = x.astype(jnp.float32)
    y = xf * lax.rsqrt(jnp.mean(xf * xf, axis=-1, keepdims=True) + RMS_EPS)
    return (y * g.astype(jnp.float32)).astype(x.dtype)


def _cmul(ar, ai, br, bi):
    return ar * br - ai * bi, ar * bi + ai * br


def _scan_combine(e1, e2):
    a1r, a1i, b1r, b1i = e1
    a2r, a2i, b2r, b2i = e2
    ar, ai = _cmul(a2r, a2i, a1r, a1i)
    br, bi = _cmul(a2r, a2i, b1r, b1i)
    return ar, ai, br + b2r, bi + b2i


def s5_mixer(u, a_re, a_im, log_dt, b_re, b_im, c_re, c_im, d_skip, w_glu, b_glu):
    f32 = jnp.float32
    bsz, seq, _ = u.shape
    lam_re = jnp.minimum(a_re.astype(f32), -1e-4)
    lam_im = a_im.astype(f32)
    dt = jnp.exp(log_dt.astype(f32))[:, None]
    mag = jnp.exp(lam_re * dt)
    abar_re = mag * jnp.cos(lam_im * dt)
    abar_im = mag * jnp.sin(lam_im * dt)
    den = lam_re * lam_re + lam_im * lam_im
    x_re = abar_re - 1.0
    f_re = (x_re * lam_re + abar_im * lam_im) / den
    f_im = (abar_im * lam_re - x_re * lam_im) / den
    bb_re, bb_im = _cmul(f_re[..., None], f_im[..., None], b_re.astype(f32), b_im.astype(f32))
    cr = c_re.astype(f32)
    ci = c_im.astype(f32)

    n_chunks = seq // SCAN_CHUNK
    uf = u.astype(f32)
    u_c = jnp.swapaxes(uf.reshape(bsz, n_chunks, SCAN_CHUNK, SSM_GROUPS, SSM_GROUP), 0, 1)
    shape_bcgn = (bsz, SCAN_CHUNK, SSM_GROUPS, SSM_STATE)
    a_seq_re = jnp.broadcast_to(abar_re, shape_bcgn)
    a_seq_im = jnp.broadcast_to(abar_im, shape_bcgn)

    def chunk_step(carry, uc):
        h0_re, h0_im = carry
        bu_re = jnp.einsum('bcgh,gnh->bcgn', uc, bb_re)
        bu_im = jnp.einsum('bcgh,gnh->bcgn', uc, bb_im)
        p_re, p_im, s_re, s_im = lax.associative_scan(
            _scan_combine, (a_seq_re, a_seq_im, bu_re, bu_im), axis=1)
        in_re, in_im = _cmul(p_re, p_im, h0_re[:, None], h0_im[:, None])
        s_re = s_re + in_re
        s_im = s_im + in_im
        y = (jnp.einsum('bcgn,ghn->bcgh', s_re, cr)
             - jnp.einsum('bcgn,ghn->bcgh', s_im, ci))
        return (s_re[:, -1], s_im[:, -1]), y

    init = (jnp.zeros((bsz, SSM_GROUPS, SSM_STATE), f32),
            jnp.zeros((bsz, SSM_GROUPS, SSM_STATE), f32))
    _, ys = lax.scan(chunk_step, init, u_c)
    y = jnp.swapaxes(ys, 0, 1).reshape(bsz, seq, SSM_WIDTH)
    y = y + uf * d_skip.astype(f32).reshape(-1)
    y = jax.nn.gelu(y).astype(u.dtype)
    return y * jax.nn.sigmoid(y @ w_glu + b_glu)


def pool_mixer(v, w_pool, b_pool, pool_scale):
    f32 = jnp.float32
    bsz, seq, _ = v.shape
    vf = v.astype(f32).reshape(bsz, seq, POOL_GROUPS, POOL_GROUP_WIDTH)
    csum = jnp.cumsum(vf, axis=1)
    pos = jnp.arange(seq)
    outs = []
    for g, win in enumerate(POOL_WINDOWS):
        cs = csum[:, :, g]
        lag = jnp.pad(cs, ((0, 0), (win, 0), (0, 0)))[:, :seq]
        cnt = jnp.minimum(pos + 1, win).astype(f32)[None, :, None]
        outs.append((cs - lag) / cnt - vf[:, :, g])
    pooled = jnp.stack(outs, axis=2).astype(v.dtype)
    mixed = jnp.einsum('bsgc,gcd->bsgd', pooled, w_pool) + b_pool
    mixed = mixed * pool_scale.reshape(POOL_GROUPS, POOL_GROUP_WIDTH)
    return mixed.reshape(bsz, seq, POOL_WIDTH)


def causal_dwconv(h, w, b):
    seq = h.shape[1]
    hp = jnp.pad(h, ((0, 0), (CONV_WIDTH - 1, 0), (0, 0)))
    out = b
    for k in range(CONV_WIDTH):
        out = out + w[k] * hp[:, k:k + seq]
    return out


def conv_gated_mlp(h, w_up, conv_w, conv_b, w_down):
    up = causal_dwconv(h @ w_up, conv_w, conv_b)
    gate, val = jnp.split(up, 2, axis=-1)
    return (jax.nn.silu(gate) * val) @ w_down


def setup_inputs(seed: int = 0) -> dict:
    key = jax.random.key(seed)
    ks = jax.random.split(key, 32)
    nrm = jax.random.normal
    f32 = jnp.float32
    L, D, G, N, H = DEPTH, D_MODEL, SSM_GROUPS, SSM_STATE, SSM_GROUP
    return {
        "x": nrm(ks[0], (BATCH, SEQ, D), f32),
        "c": nrm(ks[1], (BATCH, D), f32),
        "w_cond": nrm(ks[2], (D, N_MOD * D), f32) * (0.5 * D ** -0.5),
        "b_cond": nrm(ks[3], (N_MOD * D,), f32) * 0.01,
        "ada_table": nrm(ks[4], (L, N_MOD, D), f32) * 0.1,
        "norm1_g": 1.0 + 0.02 * nrm(ks[5], (L, D), f32),
        "norm2_g": 1.0 + 0.02 * nrm(ks[6], (L, D), f32),
        "w_in": nrm(ks[7], (L, D, IN_WIDTH), f32) * D ** -0.5,
        "ssm_a_re": -0.5 + 0.01 * nrm(ks[8], (L, G, N), f32),
        "ssm_a_im": math.pi * jnp.arange(N, dtype=f32)[None, None, :] + 0.01 * nrm(ks[9], (L, G, N), f32),
        "ssm_log_dt": jax.random.uniform(ks[10], (L, G), f32, math.log(DT_MIN), math.log(DT_MAX)),
        "ssm_b_re": nrm(ks[11], (L, G, N, H), f32) * (2 * H) ** -0.5,
        "ssm_b_im": nrm(ks[12], (L, G, N, H), f32) * (2 * H) ** -0.5,
        "ssm_c_re": nrm(ks[13], (L, G, H, N), f32) * N ** -0.5,
        "ssm_c_im": nrm(ks[14], (L, G, H, N), f32) * N ** -0.5,
        "ssm_d": nrm(ks[15], (L, G, H), f32),
        "w_glu": nrm(ks[16], (L, SSM_WIDTH, SSM_WIDTH), f32) * SSM_WIDTH ** -0.5,
        "b_glu": nrm(ks[17], (L, SSM_WIDTH), f32) * 0.01,
        "w_pool": nrm(ks[18], (L, POOL_GROUPS, POOL_GROUP_WIDTH, POOL_GROUP_WIDTH), f32) * POOL_GROUP_WIDTH ** -0.5,
        "b_pool": nrm(ks[19], (L, POOL_GROUPS, POOL_GROUP_WIDTH), f32) * 0.01,
        "pool_scale": 1.0 + 0.1 * nrm(ks[20], (L, POOL_WIDTH), f32),
        "w_ssm_out": nrm(ks[21], (L, SSM_WIDTH, D), f32) * SSM_WIDTH ** -0.5,
        "w_pool_out": nrm(ks[22], (L, POOL_WIDTH, D), f32) * POOL_WIDTH ** -0.5,
        "w_o": nrm(ks[23], (L, D, D), f32) * D ** -0.5,
        "w_up": nrm(ks[24], (L, D, 2 * D_FF), f32) * D ** -0.5,
        "conv_w": nrm(ks[25], (L, CONV_WIDTH, 2 * D_FF), f32) * CONV_WIDTH ** -0.5,
        "conv_b": nrm(ks[26], (L, 2 * D_FF), f32) * 0.01,
        "w_down": nrm(ks[27], (L, D_FF, D), f32) * D_FF ** -0.5,
        "final_g": 1.0 + 0.02 * nrm(ks[28], (D,), f32),
    }


def reference(x, c, w_cond, b_cond, ada_table, norm1_g, norm2_g, w_in,
              ssm_a_re, ssm_a_im, ssm_log_dt, ssm_b_re, ssm_b_im, ssm_c_re, ssm_c_im, ssm_d,
              w_glu, b_glu, w_pool, b_pool, pool_scale, w_ssm_out, w_pool_out, w_o,
              w_up, conv_w, conv_b, w_down, final_g):
    bsz = x.shape[0]
    cond = (jax.nn.silu(c) @ w_cond + b_cond).reshape(bsz, N_MOD, D_MODEL)
    split_at = [SSM_WIDTH, SSM_WIDTH + POOL_WIDTH, SSM_WIDTH + POOL_WIDTH + D_MODEL]
    h = x
    for l in range(DEPTH):
        mod = (cond + ada_table[l])[:, :, None, :]
        shift1, scale1, gate1 = mod[:, 0], mod[:, 1], mod[:, 2]
        shift2, scale2, gate2 = mod[:, 3], mod[:, 4], mod[:, 5]

        y = rmsnorm(h, norm1_g[l]) * (1.0 + scale1) + shift1
        proj = y @ w_in[l]
        u_ssm, u_pool, g_ssm, g_pool = jnp.split(proj, split_at, axis=-1)
        o_ssm = s5_mixer(u_ssm, ssm_a_re[l], ssm_a_im[l], ssm_log_dt[l], ssm_b_re[l], ssm_b_im[l],
                         ssm_c_re[l], ssm_c_im[l], ssm_d[l], w_glu[l], b_glu[l])
        o_pool = pool_mixer(u_pool, w_pool[l], b_pool[l], pool_scale[l])
        merged = (jax.nn.sigmoid(g_ssm) * (o_ssm @ w_ssm_out[l])
                  + jax.nn.sigmoid(g_pool) * (o_pool @ w_pool_out[l]))
        h = h + gate1 * (merged @ w_o[l])

        y = rmsnorm(h, norm2_g[l]) * (1.0 + scale2) + shift2
        h = h + gate2 * conv_gated_mlp(y, w_up[l], conv_w[l], conv_b[l], w_down[l])
    return rmsnorm(h, final_g)
```

```python
import math
from contextlib import ExitStack

import numpy as np
import concourse.bass as bass
import concourse.mybir as mybir
from concourse.bass_utils import run_bass_kernel_spmd

F32 = mybir.dt.float32
BF16 = mybir.dt.bfloat16
I32 = mybir.dt.int32
AF = mybir.ActivationFunctionType
ALU = mybir.AluOpType
P = 128
POOL_WINDOWS = (2, 4, 8, 16)
TWO_PI = 2.0 * math.pi


class Cfg:
    def __init__(self, D=4096, S=8192, L=4, TT=256):
        self.D, self.S, self.L, self.TT = D, S, L, TT
        self.KC = D // P
        self.SW = D // 2
        self.SC = self.SW // P
        self.G = self.SW // 16
        self.NP = self.G // 2
        self.PW = D // 2
        self.PC = self.PW // P
        self.PGW = self.PW // 4
        self.CPG = self.PGW // P
        self.DFF = 2 * D
        self.FC = self.DFF // P
        self.NT = S // TT
        self.NSB = TT // P
        self.NH = max(1, TT // 256)
        self.HS = self.NSB // self.NH


def wgeom(K, N):
    kc = K // P
    KH = min(16, kc)
    HB = kc // KH
    CW = min(256, N)
    NB = N // CW
    return KH, HB, CW, NB


def blockify(W):
    K, N = W.shape
    KH, HB, CW, NB = wgeom(K, N)
    return np.ascontiguousarray(
        W.reshape(HB, KH, P, NB, CW).transpose(3, 0, 2, 1, 4)).reshape(NB, HB, P, KH * CW)


def colmajor(v):
    F = v.shape[-1]
    return np.ascontiguousarray(np.swapaxes(v.reshape(v.shape[:-1] + (F // P, P)), -1, -2))


class Buf:
    __slots__ = ("name", "w", "r", "sem", "last_dma")

    def __init__(self, name):
        self.name = name
        self.w = None
        self.r = {}
        self.sem = None
        self.last_dma = None


class Sched:
    def __init__(self, nc, st):
        self.nc = nc
        self.st = st
        self.eng = {"pe": nc.tensor, "act": nc.scalar, "dve": nc.vector, "pool": nc.gpsimd, "sp": nc.sync}
        self.sem = {k: st.enter_context(nc.semaphore("sem_" + k)) for k in ("pe", "act", "dve", "pool")}
        self.cnt = {k: 0 for k in self.sem}
        self.waited = {k: {} for k in self.eng}
        self.dsems = []
        self.n_instr = 0

    def _wait(self, e, dep):
        if dep is None:
            return
        key, val = dep
        if key == "pe" and e == "pe":
            return
        if key == "pe" and val > self.cnt["pe"]:
            raise RuntimeError("wait on a PE milestone that is not emitted yet")
        w = self.waited[e]
        if w.get(key, 0) >= val:
            return
        semh = self.sem[key] if isinstance(key, str) else self.dsems[key[1]][0]
        self.eng[e].wait_ge(semh, val)
        w[key] = val
        self.n_instr += 1

    def _deps(self, e, reads, writes):
        for b in reads:
            self._wait(e, b.w)
        for b in writes:
            self._wait(e, b.w)
            for k, v in b.r.items():
                self._wait(e, (k, v))

    def _done(self, dep, reads, writes):
        k, v = dep
        for b in reads:
            if b.r.get(k, 0) < v:
                b.r[k] = v
        for b in writes:
            b.w = dep
            b.r = {}

    def op(self, e, reads, writes, fn, inc=True):
        self._deps(e, reads, writes)
        ins = fn(self.eng[e])
        self.n_instr += 1
        if e == "pe" and not inc:
            dep = ("pe", self.cnt["pe"] + 1)
        else:
            self.cnt[e] += 1
            ins.then_inc(self.sem[e], 1)
            dep = (e, self.cnt[e])
        self._done(dep, reads, writes)

    def dma(self, q, out_ap, in_ap, reads, writes, sembuf):
        if sembuf.sem is None:
            sembuf.sem = len(self.dsems)
            self.dsems.append([self.st.enter_context(self.nc.semaphore("dsem%d" % sembuf.sem)), 0])
        self._wait(q, sembuf.last_dma)
        self._deps(q, reads, writes)
        rec = self.dsems[sembuf.sem]
        rec[1] += 16
        self.eng[q].dma_start(out=out_ap, in_=in_ap).then_inc(rec[0], 16)
        self.n_instr += 1
        dep = (("dma", sembuf.sem), rec[1])
        sembuf.last_dma = dep
        self._done(dep, reads, writes)


def build(cfg, mode="fused"):
    need_cond = mode in ("fused", "cond")
    need_layer = mode in ("fused", "layer")
    need_fin = mode in ("fused", "final")
    D, S, L, TT = cfg.D, cfg.S, cfg.L, cfg.TT
    KC, SC, PC, FC, NP_, NT, NSB = cfg.KC, cfg.SC, cfg.PC, cfg.FC, cfg.NP, cfg.NT, cfg.NSB
    SW, PW, PGW, DFF = cfg.SW, cfg.PW, cfg.PGW, cfg.DFF
    NMOD = 6 * KC
    nc = bass.Bass("TRN2", target_bir_lowering=False)

    def din(name, shape, dt=F32, need=True):
        if not need:
            return None
        return nc.dram_tensor(name, list(shape), dt, kind="ExternalInput").ap()

    x_in = din("x", [S, D], need=(mode != "cond"))
    ccol = din("ccol", [P, KC], need=need_cond)
    wcond = din("wcond", [NMOD, P, KC * P], need=need_cond)
    bcond = din("bcond", [P, NMOD], need=need_cond)
    cond_in = din("cond_in", [P, NMOD], need=(mode == "layer"))
    ada = din("ada", [L, P, NMOD], need=need_layer)
    n1g = din("n1g", [L, P, KC], need=need_layer)
    n2g = din("n2g", [L, P, KC], need=need_layer)
    fing = din("fing", [P, D], need=need_fin)

    def wdin(name, K, N, lead=()):
        KH, HB, CW, NB = wgeom(K, N)
        return din(name, [L, *lead, NB, HB, P, KH * CW], need=need_layer), (KH, HB, CW, NB)

    w_u, g_u = wdin("w_u", D, SW + PW)
    w_gs, g_gs = wdin("w_gs", D, D)
    w_gp, g_gp = wdin("w_gp", D, D)
    w_glu, g_glu = wdin("w_glu", SW, SW)
    w_pool, g_pool = wdin("w_pool", PGW, PGW, lead=(4,))
    w_sso, g_sso = wdin("w_sso", SW, D)
    w_po, g_po = wdin("w_po", PW, D)
    w_o, g_o = wdin("w_o", D, D)
    w_upg, g_upg = wdin("w_upg", D, DFF)
    w_upv, g_upv = wdin("w_upv", D, DFF)
    w_dn, g_dn = wdin("w_dn", DFF, D)
    bglu = din("bglu", [L, P, SC], need=need_layer)
    bpool = din("bpool", [L, P, PC], need=need_layer)
    pscale = din("pscale", [L, P, PC], need=need_layer)
    convw = din("convw", [L, P, 3 * 2 * FC], need=need_layer)
    convb = din("convb", [L, P, 2 * FC], need=need_layer)
    s5lp = din("s5lp", [L, 3, P, NP_], need=need_layer)
    s5lr = din("s5lr", [L, 3, P, NP_ * P], need=need_layer)
    s5b = din("s5b", [L, NP_, P, 2 * P], need=need_layer)
    s5c = din("s5c", [L, NP_, P, 2 * P], need=need_layer)
    s5d = din("s5d", [L, P, SC], need=need_layer)
    jrow = din("jrow", [P, 512], need=need_layer)
    pcorr = din("pcorr", [P, 4 * 16], need=need_layer)
    ident_in = din("ident", [P, P], need=need_layer)
    if mode == "cond":
        y_out = None
        cond_out = nc.dram_tensor("cond_out", [P, NMOD], F32, kind="ExternalOutput").ap()
    else:
        y_out = nc.dram_tensor("y", [S, D], F32, kind="ExternalOutput").ap()

    hres = nc.dram_tensor("hres", [S, D], F32).ap()
    modrow = nc.dram_tensor("modrow", [L, NMOD, P], F32).ap()
    s5tab = nc.dram_tensor("s5tab", [L, NP_, P, 2 * TT], F32).ap()
    s5w = nc.dram_tensor("s5w", [L, NP_, P, 4 * P], BF16).ap()

    with ExitStack() as st:
        sc = Sched(nc, st)

        def sb(name, shape, dt=F32):
            return st.enter_context(nc.sbuf_tensor(name, list(shape), dt))

        def ps(name):
            return st.enter_context(nc.psum_tensor(name, [P, 512], F32))

        yT = sb("yT", [P, KC, TT], BF16); yT_b = Buf("yT")
        U = sb("U", [P, FC * TT], BF16)
        o_ssm = U[:, 0:SC * TT].rearrange("p (c t) -> p c t", t=TT); o_ssm_b = Buf("o_ssm")
        o_pool = U[:, SC * TT:(SC + PC) * TT].rearrange("p (c t) -> p c t", t=TT); o_pool_b = Buf("o_pool")
        mbase = (SC + PC) * TT
        merged = U[:, mbase:mbase + KC * TT].rearrange("p (c t) -> p c t", t=TT); merged_b = Buf("merged")
        ygelu = U[:, mbase:mbase + SC * TT].rearrange("p (c t) -> p c t", t=TT)
        pooled = U[:, mbase + SC * TT:mbase + (SC + PC) * TT].rearrange("p (c t) -> p c t", t=TT)
        hidden = U[:, 0:FC * TT].rearrange("p (c t) -> p c t", t=TT)
        U_bufs = [o_ssm_b, o_pool_b, merged_b]

        NWS = 4
        wslot = [sb("wslot%d" % i, [P, 16 * 256], BF16) for i in range(NWS)]
        wslot_b = [Buf("wslot%d" % i) for i in range(NWS)]
        hbuf = sb("hbuf", [P, cfg.HS, D]); hbuf_b = [Buf("hbuf%d" % i) for i in range(cfg.HS)]
        gbc = sb("gbc", [P, D]); gbc_b = Buf("gbc")
        NS5 = 2
        s5tab_s = [sb("s5tab%d" % i, [P, 2, TT]) for i in range(NS5)]
        s5w_s = [sb("s5w%d" % i, [P, 4, P], BF16) for i in range(NS5)]
        s5slot_b = [Buf("s5slot%d" % i) for i in range(NS5)]
        s5w_b = [Buf("s5wslot%d" % i) for i in range(NS5)]
        NWK = 8
        wk = sb("wk", [P, NWK, 512]); wk_b = [Buf("wk%d" % i) for i in range(NWK)]
        hbf = sb("hbf", [P, 2, TT], BF16); hbf_b = Buf("hbf")
        u_f = sb("u_f", [P, TT]); u_f_b = Buf("u_f")
        u_bf = sb("u_bf", [P, TT], BF16); u_bf_b = Buf("u_bf")
        xp = sb("xp", [P, 3, 16 + TT]); xp_b = [Buf("xp%d" % i) for i in range(3)]
        ub = sb("ub", [P, 2, 2 + TT]); ub_b = [Buf("ub0"), Buf("ub1")]
        ident = sb("ident_s", [P, P]); ident_b = Buf("ident")
        jrow_s = sb("jrow_s", [P, 512]); jrow_b = Buf("jrow")
        pcorr_s = sb("pcorr_s", [P, 4, 16]); pcorr_b = Buf("pcorr")
        modT = sb("modT", [P, NMOD]); modT_b = Buf("modT")
        condT = sb("condT", [P, NMOD]); condT_b = Buf("condT")
        AB = sb("AB", [P, 4, KC]); AB_b = Buf("AB")
        ngs = sb("ngs", [P, 2, KC]); ngs_b = Buf("ngs")
        bglu_s = sb("bglu_s", [P, SC]); bpool_s = sb("bpool_s", [P, PC]); pscale_s = sb("pscale_s", [P, PC])
        s5d_s = sb("s5d_s", [P, SC])
        convw_s = sb("convw_s", [P, 3, 2 * FC]); convb_s = sb("convb_s", [P, 2 * FC])
        prm_b = Buf("layer_params")
        rdec = sb("rdec", [P, NP_]); phir = sb("phir", [P, NP_]); s5p_b = Buf("s5p")
        sm = sb("sm", [P, 10, NP_]); sm_b = Buf("sm")
        smi = sb("smi", [P, NP_], I32)
        wki = sb("wki", [P, 512], I32); wki_b = Buf("wki")
        car_s5 = sb("car_s5", [P, 2, NP_]); car_s5_b = Buf("car_s5")
        car_pool = sb("car_pool", [P, PC, 16]); car_pool_b = Buf("car_pool")
        car_conv = sb("car_conv", [P, 2 * FC, 2]); car_conv_b = Buf("car_conv")
        stat = sb("stat", [P, 4]); stat_b = Buf("stat")
        csil = sb("csil", [P, KC]); csil_b = Buf("csil")
        wout = sb("wout", [P, 4, 4 * P], BF16); wout_b = Buf("wout")
        pst = [ps("ps%d" % i) for i in range(8)]
        pst_b = [Buf("ps%d" % i) for i in range(8)]
        PS_ACC = [0, 1, 2, 3]
        PS_B0, PS_B1, PS_Y, PS_T = 4, 5, 6, 7
        hres_b = [[Buf("hres_%d_%d" % (t, s)) for s in range(NSB)] for t in range(NT)]
        modrow_b = [Buf("modrow%d" % l) for l in range(L)]
        s5scr_b = [Buf("s5scr%d" % l) for l in range(L)]
        dram_in = Buf("dram_in")

        def act(out, in_, func, reads, writes, bias=None, scale=None, accum_out=None):
            kw = {}
            if bias is not None:
                kw["bias"] = bias
            if scale is not None:
                kw["scale"] = scale
            if accum_out is not None:
                kw["accum_out"] = accum_out
            sc.op("act", reads, writes, lambda e: e.activation(out, in_, func, **kw))

        def tt(out, a, b, op, reads, writes, e="dve"):
            sc.op(e, reads, writes, lambda en: en.tensor_tensor(out, a, b, op))

        def ts(out, a, s1, s2, op0, op1, reads, writes, e="dve"):
            if op1 is None:
                sc.op(e, reads, writes, lambda en: en.tensor_scalar(out, a, s1, None, op0))
            else:
                sc.op(e, reads, writes, lambda en: en.tensor_scalar(out, a, s1, s2, op0, op1))

        def stt(out, a, s, b, op0, op1, reads, writes):
            sc.op("dve", reads, writes, lambda en: en.scalar_tensor_tensor(out, a, s, b, op0, op1))

        def cp(out, a, reads, writes, e="dve"):
            sc.op(e, reads, writes, lambda en: en.tensor_copy(out, a))

        def mset(ap, val, writes, e="dve"):
            sc.op(e, [], writes, lambda en: en.memset(ap, val))

        def mm(out, lhsT, rhs, start, stop, reads, writes, inc):
            sc.op("pe", reads, writes, lambda en: en.matmul(out, lhsT, rhs, start=start, stop=stop), inc=inc)

        def load(dst_ap, src_ap, dst_b, src_b=dram_in, q="sp", sembuf=None):
            sc.dma(q, dst_ap, src_ap, [src_b], [dst_b], sembuf or dst_b)

        def store(dst_ap, src_ap, dst_b, src_b, q="sp"):
            sc.dma(q, dst_ap, src_ap, [src_b], [dst_b], src_b)

        def sin_turns(out, t, w0, w1, wi, reads, writes, bufs):
            rb = reads + bufs
            cp(wi, t, reads, bufs)
            cp(w0, wi, bufs, bufs)
            tt(w0, t, w0, ALU.subtract, rb, bufs)
            ts(w1, w0, 0.5, None, ALU.is_gt, None, bufs, bufs)
            tt(w0, w0, w1, ALU.subtract, bufs, bufs)
            ts(w1, w0, -0.5, None, ALU.is_lt, None, bufs, bufs)
            tt(w0, w0, w1, ALU.add, bufs, bufs)
            act(out, w0, AF.Sin, bufs, writes, scale=TWO_PI)

        out_b = Buf("y_out")
        if need_layer:
            load(ident[:], ident_in, ident_b)
            load(jrow_s[:], jrow, jrow_b)
            load(pcorr_s[:].rearrange("p a b -> p (a b)"), pcorr, pcorr_b)

        if need_cond:
            load(csil[:], ccol, csil_b)
            act(wk[:, 0, 0:KC], csil[:], AF.Sigmoid, [csil_b], [wk_b[0]])
            tt(csil[:], csil[:], wk[:, 0, 0:KC], ALU.mult, [csil_b, wk_b[0]], [csil_b])
            wc_view = [hbuf[:, 0, 0:KC * P].rearrange("p (k c) -> p k c", c=P),
                       gbc[:, 0:KC * P].rearrange("p (k c) -> p k c", c=P)]
            wc_b = [hbuf_b[0], gbc_b]
            for j in range(NMOD):
                sl = j % 2
                load(wc_view[sl].rearrange("p k c -> p (k c)"), wcond[j], wc_b[sl])
                for k in range(KC):
                    mm(pst[PS_T][:, j:j + 1], wc_view[sl][:, k, :], csil[:, k:k + 1], k == 0, k == KC - 1,
                       [wc_b[sl], csil_b], [pst_b[PS_T]], inc=(k == KC - 1))
            load(modT[:], bcond, modT_b)
            tt(condT[:], pst[PS_T][:, 0:NMOD], modT[:], ALU.add, [pst_b[PS_T], modT_b], [condT_b])
        if mode == "cond":
            store(cond_out, condT[:], out_b, condT_b)
        if mode == "layer":
            load(condT[:], cond_in, condT_b)

        def layer_setup(l):
            load(modT[:], ada[l], modT_b)
            tt(modT[:], modT[:], condT[:], ALU.add, [modT_b, condT_b], [modT_b])
            load(ngs[:, 0, :], n1g[l], ngs_b)
            load(ngs[:, 1, :], n2g[l], ngs_b)
            for i, (sh, scl) in enumerate(((0, 1), (3, 4))):
                ts(AB[:, 2 * i, :], modT[:, scl * KC:(scl + 1) * KC], 1.0, None, ALU.add, None, [modT_b], [AB_b])
                tt(AB[:, 2 * i, :], AB[:, 2 * i, :], ngs[:, i, :], ALU.mult, [AB_b, ngs_b], [AB_b])
                cp(AB[:, 2 * i + 1, :], modT[:, sh * KC:(sh + 1) * KC], [modT_b], [AB_b])
            for c0 in range(0, NMOD, P):
                cw = min(P, NMOD - c0)
                sc.op("pe", [modT_b, ident_b], [pst_b[PS_T]],
                      lambda en: en.transpose(pst[PS_T][0:cw, 0:P], modT[:, c0:c0 + cw], ident[:]))
                cp(wk[0:cw, 0, 0:P], pst[PS_T][0:cw, 0:P], [pst_b[PS_T]], [wk_b[0]])
                store(modrow[l, c0:c0 + cw, :], wk[0:cw, 0, 0:P], modrow_b[l], wk_b[0])
            for (dst, src) in ((bglu_s, bglu), (bpool_s, bpool), (pscale_s, pscale), (s5d_s, s5d), (convb_s, convb)):
                load(dst[:], src[l], prm_b)
            load(convw_s[:].rearrange("p a b -> p (a b)"), convw[l], prm_b)
            mset(car_s5[:], 0.0, [car_s5_b])
            mset(car_pool[:], 0.0, [car_pool_b])
            mset(car_conv[:], 0.0, [car_conv_b])

            for i in range(3):
                load(sm[:, i, :], s5lp[l, i], sm_b)
            smb = [sm_b]
            ts(sm[:, 0, :], sm[:, 0, :], -1e-4, None, ALU.min, None, smb, smb)
            act(sm[:, 2, :], sm[:, 2, :], AF.Exp, smb, smb)
            tt(sm[:, 3, :], sm[:, 0, :], sm[:, 2, :], ALU.mult, smb, smb)
            act(rdec[:], sm[:, 3, :], AF.Exp, smb, [s5p_b])
            tt(sm[:, 3, :], sm[:, 1, :], sm[:, 2, :], ALU.mult, smb, smb)
            ts(sm[:, 3, :], sm[:, 3, :], 1.0 / TWO_PI, None, ALU.mult, None, smb, smb)
            cp(smi[:], sm[:, 3, :], smb, smb)
            cp(sm[:, 4, :], smi[:], smb, smb)
            tt(sm[:, 4, :], sm[:, 3, :], sm[:, 4, :], ALU.subtract, smb, smb)
            ts(sm[:, 5, :], sm[:, 4, :], 0.5, None, ALU.is_gt, None, smb, smb)
            tt(sm[:, 4, :], sm[:, 4, :], sm[:, 5, :], ALU.subtract, smb, smb)
            ts(sm[:, 5, :], sm[:, 4, :], -0.5, None, ALU.is_lt, None, smb, smb)
            tt(phir[:], sm[:, 4, :], sm[:, 5, :], ALU.add, smb, [s5p_b])
            w = wk
            for p_ in range(NP_):
                sl = p_ % NS5
                tb = s5tab_s[sl]
                ts(w[:, 0, 0:TT], jrow_s[:, 0:TT], phir[:, p_:p_ + 1], None, ALU.mult, None,
                   [jrow_b, s5p_b], [wk_b[0]])
                ts(w[:, 1, 0:TT], w[:, 0, 0:TT], 0.25, None, ALU.add, None, [wk_b[0]], [wk_b[1]])
                sin_turns(tb[:, 1, :], w[:, 0, 0:TT], w[:, 2, 0:TT], w[:, 3, 0:TT], wki[:, 0:TT],
                          [wk_b[0]], [s5slot_b[sl]], [wk_b[2], wk_b[3], wki_b])
                sin_turns(tb[:, 0, :], w[:, 1, 0:TT], w[:, 2, 0:TT], w[:, 3, 0:TT], wki[:, 0:TT],
                          [wk_b[1]], [s5slot_b[sl]], [wk_b[2], wk_b[3], wki_b])
                store(s5tab[l, p_], tb[:].rearrange("p a t -> p (a t)"), s5scr_b[l], s5slot_b[sl])
            for p0 in range(0, NP_, 4):
                n = 4 * P
                wb = wk_b
                for i in range(3):
                    load(w[:, i, 0:n], s5lr[l, i, :, p0 * P:p0 * P + n], wk_b[i])
                ts(w[:, 0, 0:n], w[:, 0, 0:n], -1e-4, None, ALU.min, None, [wb[0]], [wb[0]])
                act(w[:, 2, 0:n], w[:, 2, 0:n], AF.Exp, [wb[2]], [wb[2]])
                tt(w[:, 3, 0:n], w[:, 0, 0:n], w[:, 2, 0:n], ALU.mult, [wb[0], wb[2]], [wb[3]])
                act(w[:, 3, 0:n], w[:, 3, 0:n], AF.Exp, [wb[3]], [wb[3]])
                tt(w[:, 4, 0:n], w[:, 1, 0:n], w[:, 2, 0:n], ALU.mult, [wb[1], wb[2]], [wb[4]])
                ts(w[:, 4, 0:n], w[:, 4, 0:n], 1.0 / TWO_PI, None, ALU.mult, None, [wb[4]], [wb[4]])
                ts(w[:, 5, 0:n], w[:, 4, 0:n], 0.25, None, ALU.add, None, [wb[4]], [wb[5]])
                sin_turns(w[:, 2, 0:n], w[:, 4, 0:n], w[:, 6, 0:n], w[:, 7, 0:n], wki[:, 0:n],
                          [wb[4]], [wb[2]], [wb[6], wb[7], wki_b])
                sin_turns(w[:, 4, 0:n], w[:, 5, 0:n], w[:, 6, 0:n], w[:, 7, 0:n], wki[:, 0:n],
                          [wb[5]], [wb[4]], [wb[6], wb[7], wki_b])
                tt(w[:, 2, 0:n], w[:, 2, 0:n], w[:, 3, 0:n], ALU.mult, [wb[2], wb[3]], [wb[2]])
                tt(w[:, 4, 0:n], w[:, 4, 0:n], w[:, 3, 0:n], ALU.mult, [wb[4], wb[3]], [wb[4]])
                ts(w[:, 4, 0:n], w[:, 4, 0:n], -1.0, None, ALU.add, None, [wb[4]], [wb[4]])
                tt(w[:, 3, 0:n], w[:, 0, 0:n], w[:, 0, 0:n], ALU.mult, [wb[0]], [wb[3]])
                tt(w[:, 5, 0:n], w[:, 1, 0:n], w[:, 1, 0:n], ALU.mult, [wb[1]], [wb[5]])
                tt(w[:, 3, 0:n], w[:, 3, 0:n], w[:, 5, 0:n], ALU.add, [wb[3], wb[5]], [wb[3]])
                sc.op("dve", [wb[3]], [wb[3]], lambda en: en.reciprocal(w[:, 3, 0:n], w[:, 3, 0:n]))
                tt(w[:, 5, 0:n], w[:, 4, 0:n], w[:, 0, 0:n], ALU.mult, [wb[4], wb[0]], [wb[5]])
                tt(w[:, 6, 0:n], w[:, 2, 0:n], w[:, 1, 0:n], ALU.mult, [wb[2], wb[1]], [wb[6]])
                tt(w[:, 5, 0:n], w[:, 5, 0:n], w[:, 6, 0:n], ALU.add, [wb[5], wb[6]], [wb[5]])
                tt(w[:, 5, 0:n], w[:, 5, 0:n], w[:, 3, 0:n], ALU.mult, [wb[5], wb[3]], [wb[5]])
                tt(w[:, 6, 0:n], w[:, 2, 0:n], w[:, 0, 0:n], ALU.mult, [wb[2], wb[0]], [wb[6]])
                tt(w[:, 7, 0:n], w[:, 4, 0:n], w[:, 1, 0:n], ALU.mult, [wb[4], wb[1]], [wb[7]])
                tt(w[:, 6, 0:n], w[:, 6, 0:n], w[:, 7, 0:n], ALU.subtract, [wb[6], wb[7]], [wb[6]])
                tt(w[:, 6, 0:n], w[:, 6, 0:n], w[:, 3, 0:n], ALU.mult, [wb[6], wb[3]], [wb[6]])
                braw = hbuf[:, 0, 0:8 * P].rearrange("p (a r q) -> p a r q", r=2, q=P)
                craw = gbc[:, 0:8 * P].rearrange("p (a r q) -> p a r q", r=2, q=P)
                for a in range(4):
                    load(hbuf[:, 0, a * 2 * P:(a + 1) * 2 * P], s5b[l, p0 + a], hbuf_b[0])
                    load(gbc[:, a * 2 * P:(a + 1) * 2 * P], s5c[l, p0 + a], gbc_b)
                fre = w[:, 5, 0:n].rearrange("p (a q) -> p a q", q=P)
                fim = w[:, 6, 0:n].rearrange("p (a q) -> p a q", q=P)
                t0 = w[:, 0, 0:n].rearrange("p (a q) -> p a q", q=P)
                t1 = w[:, 1, 0:n].rearrange("p (a q) -> p a q", q=P)
                hb0 = [hbuf_b[0]]
                tt(t0, fre, braw[:, :, 0, :], ALU.mult, [wb[5]] + hb0, [wb[0]])
                tt(t1, fim, braw[:, :, 1, :], ALU.mult, [wb[6]] + hb0, [wb[1]])
                tt(wout[:, :, 0:P], t0, t1, ALU.subtract, [wb[0], wb[1]], [wout_b])
                tt(t0, fre, braw[:, :, 1, :], ALU.mult, [wb[5]] + hb0, [wb[0]])
                tt(t1, fim, braw[:, :, 0, :], ALU.mult, [wb[6]] + hb0, [wb[1]])
                tt(wout[:, :, P:2 * P], t0, t1, ALU.add, [wb[0], wb[1]], [wout_b])
                cp(wout[:, :, 2 * P:3 * P], craw[:, :, 0, :], [gbc_b], [wout_b])
                ts(wout[:, :, 3 * P:4 * P], craw[:, :, 1, :], -1.0, None, ALU.mult, None, [gbc_b], [wout_b])
                for a in range(4):
                    store(s5w[l, p0 + a], wout[:, a, :], s5scr_b[l], wout_b)

        ws_rr = [0]

        def load_wblock(wap, geom, idx, nb, hb):
            KH, HB, CW, NB = geom
            s = ws_rr[0] % NWS
            ws_rr[0] += 1
            src = wap[idx][nb, hb] if not isinstance(idx, tuple) else wap[idx[0], idx[1]][nb, hb]
            sc.dma("pool", wslot[s][:, 0:KH * CW], src, [dram_in], [wslot_b[s]], wslot_b[s])
            return s, wslot[s][:, 0:KH * CW].rearrange("p (k c) -> p k c", c=CW)

        acc_rr = [0]

        def _l(b):
            return list(b) if isinstance(b, (list, tuple)) else [b]

        def fm_block(wap, geom, idx, nb, actT, act_b, kofs=0):
            KH, HB, CW, NB = geom
            noc = CW // P
            banks = [PS_ACC[(acc_rr[0] + i) % 4] for i in range(noc)]
            acc_rr[0] += noc
            Kc = KH * HB
            for hb in range(HB):
                s, wv = load_wblock(wap, geom, idx, nb, hb)
                for oc in range(noc):
                    for kc in range(KH):
                        k = hb * KH + kc
                        last = (k == Kc - 1)
                        lastread = (oc == noc - 1 and kc == KH - 1)
                        mm(pst[banks[oc]][:, 0:TT], wv[:, kc, oc * P:(oc + 1) * P], actT[:, kofs + k, :],
                           k == 0, last, [wslot_b[s]] + _l(act_b), [pst_b[banks[oc]]], inc=(last or lastread))
            return banks

        def norm_to_yT(half, AB_i, hb_list):
            for si in range(cfg.HS):
                hv = hbuf[:, si, :]
                hb_ = hbuf_b[si]
                col0 = (half * cfg.HS + si) * P
                mset(stat[:, 0:1], 0.0, [stat_b])
                for c0 in range(0, D, 4096):
                    cw = min(4096, D - c0)
                    junk = wk[:].rearrange("p a b -> p (a b)")[:, 0:cw]
                    act(junk, hv[:, c0:c0 + cw], AF.Square, [hb_], wk_b + [stat_b], accum_out=stat[:, 1:2])
                    tt(stat[:, 0:1], stat[:, 0:1], stat[:, 1:2], ALU.add, [stat_b], [stat_b])
                ts(stat[:, 2:3], stat[:, 0:1], 1.0 / D, 1e-6, ALU.mult, ALU.add, [stat_b], [stat_b])
                act(stat[:, 2:3], stat[:, 2:3], AF.Sqrt, [stat_b], [stat_b])
                sc.op("dve", [stat_b], [stat_b], lambda en: en.reciprocal(stat[:, 3:4], stat[:, 2:3]))
                ts(hv, hv, stat[:, 3:4], None, ALU.mult, None, [hb_, stat_b], [hb_])
                for k0 in range(0, KC, 4):
                    for j in range(4):
                        k = k0 + j
                        sc.op("pe", [hb_, ident_b], [pst_b[PS_T]],
                              lambda en: en.transpose(pst[PS_T][:, j * P:(j + 1) * P], hv[:, k * P:(k + 1) * P], ident[:]))
                    for j in range(4):
                        k = k0 + j
                        act(yT[:, k, col0:col0 + P], pst[PS_T][:, j * P:(j + 1) * P], AF.Identity,
                            [pst_b[PS_T], AB_b], [yT_b], bias=AB[:, AB_i + 1, k:k + 1], scale=AB[:, AB_i, k:k + 1])

        def tm_stage(l, t, wap, geom, actT, act_b, gate_idx, after, from_x=False):
            KH, HB, CW, NB = geom
            Kc = KH * HB
            load(gbc[:], modrow[l, gate_idx * KC:(gate_idx + 1) * KC, :].rearrange("c p -> (c p)").partition_broadcast(P),
                 gbc_b, modrow_b[l])
            for half in range(cfg.NH):
                for si in range(cfg.HS):
                    s_ = half * cfg.HS + si
                    r0 = t * TT + s_ * P
                    if from_x:
                        load(hbuf[:, si, :], x_in[r0:r0 + P, :], hbuf_b[si])
                    else:
                        load(hbuf[:, si, :], hres[r0:r0 + P, :], hbuf_b[si], hres_b[t][s_])
                for nb in range(NB):
                    banks = [PS_ACC[(acc_rr[0] + i) % 4] for i in range(cfg.HS)]
                    acc_rr[0] += cfg.HS
                    for hb in range(HB):
                        s, wv = load_wblock(wap, geom, l, nb, hb)
                        for si in range(cfg.HS):
                            tcol = (half * cfg.HS + si) * P
                            for kc in range(KH):
                                k = hb * KH + kc
                                last = (k == Kc - 1)
                                lastread = (si == cfg.HS - 1 and kc == KH - 1)
                                mm(pst[banks[si]][:, 0:CW], actT[:, k, tcol:tcol + P], wv[:, kc, :],
                                   k == 0, last, [wslot_b[s]] + _l(act_b), [pst_b[banks[si]]], inc=(last or lastread))
                    for si in range(cfg.HS):
                        tmp = wk[:, si, 0:CW]
                        tt(tmp, pst[banks[si]][:, 0:CW], gbc[:, nb * CW:(nb + 1) * CW], ALU.mult,
                           [pst_b[banks[si]], gbc_b], [wk_b[si]])
                        tt(hbuf[:, si, nb * CW:(nb + 1) * CW], hbuf[:, si, nb * CW:(nb + 1) * CW], tmp, ALU.add,
                           [hbuf_b[si], wk_b[si]], [hbuf_b[si]], e="pool")
                after(half)

        def store_h(t, half, to_y=False):
            for si in range(cfg.HS):
                s_ = half * cfg.HS + si
                r0 = t * TT + s_ * P
                if to_y:
                    store(y_out[r0:r0 + P, :], hbuf[:, si, :], out_b, hbuf_b[si])
                else:
                    store(hres[r0:r0 + P, :], hbuf[:, si, :], hres_b[t][s_], hbuf_b[si])

        def final_norm_store(t, half):
            for si in range(cfg.HS):
                hv = hbuf[:, si, :]
                hb_ = hbuf_b[si]
                s_ = half * cfg.HS + si
                r0 = t * TT + s_ * P
                mset(stat[:, 0:1], 0.0, [stat_b])
                for c0 in range(0, D, 4096):
                    cw = min(4096, D - c0)
                    junk = wk[:].rearrange("p a b -> p (a b)")[:, 0:cw]
                    act(junk, hv[:, c0:c0 + cw], AF.Square, [hb_], wk_b + [stat_b], accum_out=stat[:, 1:2])
                    tt(stat[:, 0:1], stat[:, 0:1], stat[:, 1:2], ALU.add, [stat_b], [stat_b])
                ts(stat[:, 2:3], stat[:, 0:1], 1.0 / D, 1e-6, ALU.mult, ALU.add, [stat_b], [stat_b])
                act(stat[:, 2:3], stat[:, 2:3], AF.Sqrt, [stat_b], [stat_b])
                sc.op("dve", [stat_b], [stat_b], lambda en: en.reciprocal(stat[:, 3:4], stat[:, 2:3]))
                stt(hv, hv, stat[:, 3:4], gbc[:], ALU.mult, ALU.mult, [hb_, stat_b, gbc_b], [hb_])
                store(y_out[r0:r0 + P, :], hv, out_b, hb_)

        def mixer(l, t):
            for half in range(cfg.NH):
                for si in range(cfg.HS):
                    s_ = half * cfg.HS + si
                    r0 = t * TT + s_ * P
                    if l == 0:
                        load(hbuf[:, si, :], x_in[r0:r0 + P, :], hbuf_b[si])
                    else:
                        load(hbuf[:, si, :], hres[r0:r0 + P, :], hbuf_b[si], hres_b[t][s_])
                norm_to_yT(half, 0, None)
            build.chk("m_norm")
            for nb in range(g_u[3]):
                banks = fm_block(w_u, g_u, l, nb, yT, yT_b)
                for oc, bk in enumerate(banks):
                    ch = nb * (g_u[2] // P) + oc
                    if ch < SC:
                        if getattr(cfg, "stop", None) == "m_fm":
                            build.chk("m_fm")
                        s5_chunk(l, t, ch, bk)
                        build.chk("m_s5")
                    else:
                        pool_chunk(l, t, ch - SC, bk)
                        build.chk("m_pool")
            for nb in range(g_glu[3]):
                banks = fm_block(w_glu, g_glu, l, nb, ygelu, merged_b)
                for oc, bk in enumerate(banks):
                    ch = nb * (g_glu[2] // P) + oc
                    act(wk[:, 0, 0:TT], pst[bk][:, 0:TT], AF.Sigmoid, [pst_b[bk], prm_b], [wk_b[0]],
                        bias=bglu_s[:, ch:ch + 1])
                    tt(o_ssm[:, ch, :], ygelu[:, ch, :], wk[:, 0, 0:TT], ALU.mult, [merged_b, wk_b[0]], [o_ssm_b])
            build.chk("m_glu")
            for gi in range(4):
                for nb in range(g_pool[3]):
                    banks = fm_block(w_pool, g_pool, (l, gi), nb, pooled, merged_b, kofs=gi * cfg.CPG)
                    for oc, bk in enumerate(banks):
                        ch = gi * cfg.CPG + nb * (g_pool[2] // P) + oc
                        ts(o_pool[:, ch, :], pst[bk][:, 0:TT], bpool_s[:, ch:ch + 1], pscale_s[:, ch:ch + 1],
                           ALU.add, ALU.mult, [pst_b[bk], prm_b], [o_pool_b])
            build.chk("m_pmix")
            for nb in range(g_gs[3]):
                noc = g_gs[2] // P
                banks = fm_block(w_gs, g_gs, l, nb, yT, yT_b)
                for oc, bk in enumerate(banks):
                    act(wk[:, oc, 0:TT], pst[bk][:, 0:TT], AF.Sigmoid, [pst_b[bk]], [wk_b[oc]])
                banks = fm_block(w_sso, g_sso, l, nb, o_ssm, o_ssm_b)
                for oc, bk in enumerate(banks):
                    tt(wk[:, oc, 0:TT], pst[bk][:, 0:TT], wk[:, oc, 0:TT], ALU.mult, [pst_b[bk], wk_b[oc]], [wk_b[oc]])
                banks = fm_block(w_gp, g_gp, l, nb, yT, yT_b)
                for oc, bk in enumerate(banks):
                    act(wk[:, 2 + oc, 0:TT], pst[bk][:, 0:TT], AF.Sigmoid, [pst_b[bk]], [wk_b[2 + oc]])
                banks = fm_block(w_po, g_po, l, nb, o_pool, o_pool_b)
                for oc, bk in enumerate(banks):
                    tt(wk[:, 2 + oc, 0:TT], pst[bk][:, 0:TT], wk[:, 2 + oc, 0:TT], ALU.mult,
                       [pst_b[bk], wk_b[2 + oc]], [wk_b[2 + oc]])
                    tt(merged[:, nb * noc + oc, :], wk[:, oc, 0:TT], wk[:, 2 + oc, 0:TT], ALU.add,
                       [wk_b[oc], wk_b[2 + oc]], [merged_b], e="pool")

            build.chk("m_merge")
            def after(half):
                store_h(t, half)
                norm_to_yT(half, 2, None)
            tm_stage(l, t, w_o, g_o, merged, merged_b, 2, after, from_x=(l == 0))

        import os
        PE_ = "dve" if os.environ.get("S5_NOPOOL") else "pool"
        NOSCAN = bool(os.environ.get("S5_NOSCAN"))

        def s5_chunk(l, t, ch, bk):
            cp(u_f[:], pst[bk][:, 0:TT], [pst_b[bk]], [u_f_b], e="dve")
            act(u_bf[:], u_f[:], AF.Identity, [u_f_b], [u_bf_b])
            build.chk('s5a')
            w = wk
            for j in range(4):
                p_ = ch * 4 + j
                sl = p_ % NS5
                load(s5tab_s[sl][:].rearrange("p a t -> p (a t)"), s5tab[l, p_], s5slot_b[sl], s5scr_b[l])
                load(s5w_s[sl][:].rearrange("p a q -> p (a q)"), s5w[l, p_], s5w_b[sl], s5scr_b[l])
                build.chk('s5b')
                cs_, sn_ = s5tab_s[sl][:, 0, :], s5tab_s[sl][:, 1, :]
                tb = s5slot_b[sl]
                mm(pst[PS_B0][:, 0:TT], s5w_s[sl][:, 0, :], u_bf[:], True, True, [s5w_b[sl], u_bf_b], [pst_b[PS_B0]], True)
                mm(pst[PS_B1][:, 0:TT], s5w_s[sl][:, 1, :], u_bf[:], True, True, [s5w_b[sl], u_bf_b], [pst_b[PS_B1]], True)
                build.chk('s5c')
                bre, bim = pst[PS_B0][:, 0:TT], pst[PS_B1][:, 0:TT]
                tt(w[:, 0, 0:TT], bre, cs_, ALU.mult, [pst_b[PS_B0], tb], [wk_b[0]])
                tt(w[:, 1, 0:TT], bim, sn_, ALU.mult, [pst_b[PS_B1], tb], [wk_b[1]])
                tt(w[:, 2, 0:TT], w[:, 0, 0:TT], w[:, 1, 0:TT], ALU.add, [wk_b[0], wk_b[1]], [wk_b[2]], e=PE_)
                tt(w[:, 0, 0:TT], bim, cs_, ALU.mult, [pst_b[PS_B1], tb], [wk_b[0]])
                tt(w[:, 1, 0:TT], bre, sn_, ALU.mult, [pst_b[PS_B0], tb], [wk_b[1]])
                tt(w[:, 3, 0:TT], w[:, 0, 0:TT], w[:, 1, 0:TT], ALU.subtract, [wk_b[0], wk_b[1]], [wk_b[3]], e=PE_)
                build.chk('s5d')
                dec = rdec[:, p_:p_ + 1].to_broadcast([P, TT])
                if NOSCAN:
                    cp(w[:, 4, 0:TT], w[:, 2, 0:TT], [wk_b[2]], [wk_b[4]])
                    cp(w[:, 5, 0:TT], w[:, 3, 0:TT], [wk_b[3]], [wk_b[5]])
                else:
                    sc.op("dve", [wk_b[2], s5p_b, car_s5_b], [wk_b[4]],
                          lambda en: en.tensor_tensor_scan(w[:, 4, 0:TT], dec, w[:, 2, 0:TT], car_s5[:, 0, p_:p_ + 1], ALU.mult, ALU.add))
                    sc.op("dve", [wk_b[3], s5p_b, car_s5_b], [wk_b[5]],
                          lambda en: en.tensor_tensor_scan(w[:, 5, 0:TT], dec, w[:, 3, 0:TT], car_s5[:, 1, p_:p_ + 1], ALU.mult, ALU.add))
                build.chk('s5e')
                gre, gim = w[:, 4, 0:TT], w[:, 5, 0:TT]
                tt(w[:, 0, 0:TT], gre, cs_, ALU.mult, [wk_b[4], tb], [wk_b[0]])
                tt(w[:, 1, 0:TT], gim, sn_, ALU.mult, [wk_b[5], tb], [wk_b[1]], e=PE_)
                tt(w[:, 2, 0:TT], w[:, 0, 0:TT], w[:, 1, 0:TT], ALU.subtract, [wk_b[0], wk_b[1]], [wk_b[2]])
                tt(w[:, 6, 0:TT], gre, sn_, ALU.mult, [wk_b[4], tb], [wk_b[6]], e=PE_)
                tt(w[:, 7, 0:TT], gim, cs_, ALU.mult, [wk_b[5], tb], [wk_b[7]])
                tt(w[:, 3, 0:TT], w[:, 6, 0:TT], w[:, 7, 0:TT], ALU.add, [wk_b[6], wk_b[7]], [wk_b[3]])
                build.chk('s5f')
                cp(car_s5[:, 0, p_:p_ + 1], w[:, 2, TT - 1:TT], [wk_b[2]], [car_s5_b], e=PE_)
                cp(car_s5[:, 1, p_:p_ + 1], w[:, 3, TT - 1:TT], [wk_b[3]], [car_s5_b], e=PE_)
                act(hbf[:, 0, :], w[:, 2, 0:TT], AF.Identity, [wk_b[2]], [hbf_b])
                act(hbf[:, 1, :], w[:, 3, 0:TT], AF.Identity, [wk_b[3]], [hbf_b])
                build.chk('s5g')
                mm(pst[PS_Y][:, 0:TT], s5w_s[sl][:, 2, :], hbf[:, 0, :], j == 0, False, [s5w_b[sl], hbf_b], [pst_b[PS_Y]], True)
                mm(pst[PS_Y][:, 0:TT], s5w_s[sl][:, 3, :], hbf[:, 1, :], False, j == 3, [s5w_b[sl], hbf_b], [pst_b[PS_Y]], True)
            yv = w[:, 0, 0:TT]
            stt(yv, u_f[:], s5d_s[:, ch:ch + 1], pst[PS_Y][:, 0:TT], ALU.mult, ALU.add, [u_f_b, prm_b, pst_b[PS_Y]], [wk_b[0]])
            tt(w[:, 1, 0:TT], yv, yv, ALU.mult, [wk_b[0]], [wk_b[1]])
            ts(w[:, 1, 0:TT], w[:, 1, 0:TT], 0.044715, 1.0, ALU.mult, ALU.add, [wk_b[1]], [wk_b[1]])
            tt(w[:, 1, 0:TT], w[:, 1, 0:TT], yv, ALU.mult, [wk_b[1], wk_b[0]], [wk_b[1]])
            act(w[:, 1, 0:TT], w[:, 1, 0:TT], AF.Sigmoid, [wk_b[1]], [wk_b[1]], scale=2.0 * math.sqrt(2.0 / math.pi))
            tt(ygelu[:, ch, :], yv, w[:, 1, 0:TT], ALU.mult, [wk_b[0], wk_b[1]], [merged_b])

        def pool_chunk(l, t, pc, bk):
            wi = pc // cfg.CPG
            win = POOL_WINDOWS[wi]
            X, A, B = xp[:, 0, :], xp[:, 1, :], xp[:, 2, :]
            cp(X[:, 0:16], car_pool[:, pc, :], [car_pool_b], [xp_b[0]], e="pool")
            act(X[:, 16:16 + TT], pst[bk][:, 0:TT], AF.Identity, [pst_b[bk]], [xp_b[0]])
            n = 16 + TT
            tt(A[:, 1:n], X[:, 1:n], X[:, 0:n - 1], ALU.add, [xp_b[0]], [xp_b[1]], e="pool")
            cur, cur_b, oth, oth_b = A, xp_b[1], B, xp_b[2]
            sh = 2
            while sh < win:
                lo = 2 * sh - 1
                tt(oth[:, lo:n], cur[:, lo:n], cur[:, lo - sh:n - sh], ALU.add, [cur_b], [oth_b], e="pool")
                cur, cur_b, oth, oth_b = oth, oth_b, cur, cur_b
                sh *= 2
            if t == 0:
                tt(cur[:, 16:32], cur[:, 16:32], pcorr_s[:, wi, :], ALU.mult, [cur_b, pcorr_b], [cur_b], e="pool")
            stt(pooled[:, pc, :], cur[:, 16:16 + TT], 1.0 / win, X[:, 16:16 + TT], ALU.mult, ALU.subtract,
                [cur_b, xp_b[0]], [merged_b])
            cp(car_pool[:, pc, :], X[:, TT:TT + 16], [xp_b[0]], [car_pool_b], e="pool")

        def mlp(l, t, last_layer):
            w = wk
            for nb in range(g_upg[3]):
                noc = g_upg[2] // P
                bg = fm_block(w_upg, g_upg, l, nb, yT, yT_b)
                bv = fm_block(w_upv, g_upv, l, nb, yT, yT_b)
                for oc in range(noc):
                    j = nb * noc + oc
                    accs = []
                    for which, bk, fi in ((0, bg[oc], j), (1, bv[oc], FC + j)):
                        u = ub[:, which, :]
                        cp(u[:, 0:2], car_conv[:, fi, :], [car_conv_b], [ub_b[which]], e="pool")
                        act(u[:, 2:2 + TT], pst[bk][:, 0:TT], AF.Identity, [pst_b[bk]], [ub_b[which]])
                        a_ = w[:, 4 * which, 0:TT]
                        ab = wk_b[4 * which]
                        act(a_, u[:, 2:2 + TT], AF.Identity, [ub_b[which], prm_b], [ab],
                            bias=convb_s[:, fi:fi + 1], scale=convw_s[:, 2, fi:fi + 1])
                        stt(a_, u[:, 1:1 + TT], convw_s[:, 1, fi:fi + 1], a_, ALU.mult, ALU.add, [ub_b[which], prm_b, ab], [ab])
                        stt(a_, u[:, 0:TT], convw_s[:, 0, fi:fi + 1], a_, ALU.mult, ALU.add, [ub_b[which], prm_b, ab], [ab])
                        cp(car_conv[:, fi, :], u[:, TT:TT + 2], [ub_b[which]], [car_conv_b], e="pool")
                        accs.append((a_, ab))
                    (ag, agb), (av, avb) = accs
                    act(w[:, 1, 0:TT], ag, AF.Sigmoid, [agb], [wk_b[1]])
                    tt(w[:, 1, 0:TT], w[:, 1, 0:TT], ag, ALU.mult, [wk_b[1], agb], [wk_b[1]], e="pool")
                    tt(hidden[:, j, :], w[:, 1, 0:TT], av, ALU.mult, [wk_b[1], avb], U_bufs)

            def after(half):
                if not last_layer:
                    store_h(t, half, to_y=(mode == "layer"))
                    return
                for si in range(cfg.HS):
                    hv = hbuf[:, si, :]
                    hb_ = hbuf_b[si]
                    s_ = half * cfg.HS + si
                    r0 = t * TT + s_ * P
                    mset(stat[:, 0:1], 0.0, [stat_b])
                    for c0 in range(0, D, 4096):
                        cw = min(4096, D - c0)
                        junk = wk[:].rearrange("p a b -> p (a b)")[:, 0:cw]
                        act(junk, hv[:, c0:c0 + cw], AF.Square, [hb_], wk_b + [stat_b], accum_out=stat[:, 1:2])
                        tt(stat[:, 0:1], stat[:, 0:1], stat[:, 1:2], ALU.add, [stat_b], [stat_b])
                    ts(stat[:, 2:3], stat[:, 0:1], 1.0 / D, 1e-6, ALU.mult, ALU.add, [stat_b], [stat_b])
                    act(stat[:, 2:3], stat[:, 2:3], AF.Sqrt, [stat_b], [stat_b])
                    sc.op("dve", [stat_b], [stat_b], lambda en: en.reciprocal(stat[:, 3:4], stat[:, 2:3]))
                    stt(hv, hv, stat[:, 3:4], gbc[:], ALU.mult, ALU.mult, [hb_, stat_b, gbc_b], [hb_])
                    store(y_out[r0:r0 + P, :], hv, out_b, hb_)
            if last_layer:
                pass
            tm_stage_mlp(l, t, after, last_layer)

        def tm_stage_mlp(l, t, after, last_layer):
            if not last_layer:
                tm_stage(l, t, w_dn, g_dn, hidden, U_bufs, 5, after)
                return

            def after2(half):
                load(gbc[:], fing, gbc_b)
                after(half)
                if half + 1 < cfg.NH:
                    load(gbc[:], modrow[l, 5 * KC:6 * KC, :].rearrange("c p -> (c p)").partition_broadcast(P),
                         gbc_b, modrow_b[l])
            tm_stage(l, t, w_dn, g_dn, hidden, U_bufs, 5, after2)


        class _Stop(Exception):
            pass

        def chk(name):
            if getattr(cfg, "stop", None) == name:
                raise _Stop()
        build.chk = chk
        try:
            chk("cond")
            if mode == "final":
                load(gbc[:], fing, gbc_b)
                for t in range(NT):
                    for half in range(cfg.NH):
                        for si in range(cfg.HS):
                            r0 = t * TT + (half * cfg.HS + si) * P
                            load(hbuf[:, si, :], x_in[r0:r0 + P, :], hbuf_b[si])
                        final_norm_store(t, half)
            for l in range(L if need_layer else 0):
                layer_setup(l)
                chk("setup")
                for t in range(NT):
                    mixer(l, t)
                    chk("mixer")
                    mlp(l, t, (l == L - 1) and mode == "fused")
                    chk("mlp")
        except _Stop:
            pass

        for e in ("sp",):
            sc._wait(e, out_b.w)
            for k, v in list(out_b.r.items()):
                sc._wait(e, (k, v))
        build.n_instr = sc.n_instr
    return nc


PER_LAYER_KEYS = ["ada_table", "norm1_g", "norm2_g", "w_in", "ssm_a_re", "ssm_a_im", "ssm_log_dt", "ssm_b_re",
                  "ssm_b_im", "ssm_c_re", "ssm_c_im", "ssm_d", "w_glu", "b_glu", "w_pool", "b_pool", "pool_scale",
                  "w_ssm_out", "w_pool_out", "w_o", "w_up", "conv_w", "conv_b", "w_down"]


def _f(a):
    return np.ascontiguousarray(np.asarray(a, dtype=np.float32))


def prep_cond(cfg, inp, b):
    KC = cfg.KC
    m = {}
    m["ccol"] = _f(colmajor(_f(inp["c"][b])))
    wc = _f(inp["w_cond"])
    m["wcond"] = np.ascontiguousarray(
        wc.reshape(KC, P, 6 * KC, P).transpose(2, 1, 0, 3)).reshape(6 * KC, P, KC * P)
    m["bcond"] = _f(colmajor(_f(inp["b_cond"])))
    return m


def prep_final(cfg, inp):
    return {"fing": np.ascontiguousarray(np.broadcast_to(_f(inp["final_g"])[None, :], (P, cfg.D)))}


def prep_layer(cfg, inp):
    D, L = cfg.D, cfg.L
    KC, NP_, SW, PW = cfg.KC, cfg.NP, cfg.SW, cfg.PW
    f = _f
    m = {}
    m["ada"] = f(colmajor(f(inp["ada_table"]).reshape(L, 6 * D)))
    m["n1g"] = f(colmajor(f(inp["norm1_g"])))
    m["n2g"] = f(colmajor(f(inp["norm2_g"])))
    w_in = f(inp["w_in"])
    m["w_u"] = np.stack([blockify(w_in[l][:, :SW + PW]) for l in range(L)])
    m["w_gs"] = np.stack([blockify(w_in[l][:, SW + PW:SW + PW + D]) for l in range(L)])
    m["w_gp"] = np.stack([blockify(w_in[l][:, SW + PW + D:]) for l in range(L)])
    m["w_glu"] = np.stack([blockify(f(inp["w_glu"][l])) for l in range(L)])
    m["w_pool"] = np.stack([np.stack([blockify(f(inp["w_pool"][l, g])) for g in range(4)]) for l in range(L)])
    m["w_sso"] = np.stack([blockify(f(inp["w_ssm_out"][l])) for l in range(L)])
    m["w_po"] = np.stack([blockify(f(inp["w_pool_out"][l])) for l in range(L)])
    m["w_o"] = np.stack([blockify(f(inp["w_o"][l])) for l in range(L)])
    w_up = f(inp["w_up"])
    m["w_upg"] = np.stack([blockify(w_up[l][:, :cfg.DFF]) for l in range(L)])
    m["w_upv"] = np.stack([blockify(w_up[l][:, cfg.DFF:]) for l in range(L)])
    m["w_dn"] = np.stack([blockify(f(inp["w_down"][l])) for l in range(L)])
    m["bglu"] = f(colmajor(f(inp["b_glu"])))
    m["bpool"] = f(colmajor(f(inp["b_pool"]).reshape(L, PW)))
    m["pscale"] = f(colmajor(f(inp["pool_scale"])))
    m["convw"] = f(colmajor(f(inp["conv_w"])).transpose(0, 2, 1, 3)).reshape(L, P, 3 * 2 * cfg.FC)
    m["convb"] = f(colmajor(f(inp["conv_b"])))
    are, aim, ldt = f(inp["ssm_a_re"]), f(inp["ssm_a_im"]), f(inp["ssm_log_dt"])
    ldt_b = np.broadcast_to(ldt[:, :, None], are.shape)
    lp = np.stack([are, aim, ldt_b], axis=1)
    lp = lp.reshape(L, 3, NP_, 2 * 64)
    m["s5lp"] = np.ascontiguousarray(lp.transpose(0, 1, 3, 2))
    m["s5lr"] = np.ascontiguousarray(np.broadcast_to(lp.reshape(L, 3, 1, NP_ * P), (L, 3, P, NP_ * P)))
    bre, bim = f(inp["ssm_b_re"]), f(inp["ssm_b_im"])
    cre, cim = f(inp["ssm_c_re"]), f(inp["ssm_c_im"])
    s5b = np.zeros((L, NP_, P, 2, P), np.float32)
    s5c = np.zeros((L, NP_, P, 2, P), np.float32)
    for p_ in range(NP_):
        base = 32 * (p_ % 4)
        for two in range(2):
            g = 2 * p_ + two
            rows = slice(base + 16 * two, base + 16 * two + 16)
            qs = slice(64 * two, 64 * two + 64)
            s5b[:, p_, rows, 0, qs] = bre[:, g].transpose(0, 2, 1)
            s5b[:, p_, rows, 1, qs] = bim[:, g].transpose(0, 2, 1)
            s5c[:, p_, qs, 0, rows] = cre[:, g].transpose(0, 2, 1)
            s5c[:, p_, qs, 1, rows] = cim[:, g].transpose(0, 2, 1)
    m["s5b"] = s5b.reshape(L, NP_, P, 2 * P)
    m["s5c"] = s5c.reshape(L, NP_, P, 2 * P)
    m["s5d"] = f(colmajor(f(inp["ssm_d"]).reshape(L, SW)))
    m["jrow"] = np.ascontiguousarray(np.broadcast_to(np.arange(1, 513, dtype=np.float32)[None, :], (P, 512)))
    pc = np.ones((4, 16), np.float32)
    for wi, wn in enumerate(POOL_WINDOWS):
        for t in range(16):
            pc[wi, t] = wn / min(t + 1, wn)
    m["pcorr"] = np.ascontiguousarray(np.broadcast_to(pc.reshape(1, 64), (P, 64)))
    m["ident"] = np.eye(P, dtype=np.float32)
    return m


def run_fused(cfg, inputs, trace=False):
    nb = inputs["x"].shape[0]
    nc = build(cfg, "fused")
    lay = prep_layer(cfg, inputs)
    fin = prep_final(cfg, inputs)
    in_maps = [dict(lay, **fin, **prep_cond(cfg, inputs, b), x=_f(inputs["x"][b])) for b in range(nb)]
    res = run_bass_kernel_spmd(nc, in_maps, core_ids=list(range(nb)), trace=trace)
    out = np.stack([res.results[b]["y"] for b in range(nb)], axis=0)
    return out.astype(np.float32), res


def run_multi(cfg1, inputs, depth, trace=False):
    nb = inputs["x"].shape[0]
    cores = list(range(nb))
    ncA = build(cfg1, "cond")
    resA = run_bass_kernel_spmd(ncA, [prep_cond(cfg1, inputs, b) for b in range(nb)], core_ids=cores, trace=trace)
    cond = [np.ascontiguousarray(resA.results[b]["cond_out"]) for b in range(nb)]
    h = [_f(inputs["x"][b]) for b in range(nb)]
    ncB = build(cfg1, "layer")
    for l in range(depth):
        lay = prep_layer(cfg1, {k: np.asarray(inputs[k])[l:l + 1] for k in PER_LAYER_KEYS})
        resB = run_bass_kernel_spmd(ncB, [dict(lay, x=h[b], cond_in=cond[b]) for b in range(nb)],
                                    core_ids=cores, trace=trace)
        h = [np.ascontiguousarray(resB.results[b]["y"]) for b in range(nb)]
        del lay
    ncC = build(cfg1, "final")
    fin = prep_final(cfg1, inputs)
    resC = run_bass_kernel_spmd(ncC, [dict(fin, x=h[b]) for b in range(nb)], core_ids=cores, trace=trace)
    out = np.stack([resC.results[b]["y"] for b in range(nb)], axis=0)
    return out.astype(np.float32), resC


def run(cfg, inputs, trace=False):
    return run_fused(cfg, inputs, trace)


def kernel(**inputs):
    cfg1 = Cfg(D=4096, S=8192, L=1, TT=256)
    out, _ = run_multi(cfg1, inputs, 4)
    return out
```
